# Optimizing a Trainium2 kernel written in Bass

```python
import jax, jax.numpy as jnp
from jax import lax
import numpy as np

D_MODEL = 2048
BATCH = 4
SEQ = 4096
DEPTH = 4

PLE_DIM = 256
D_FF = 4 * D_MODEL
EPS = 1e-6
NEG = -1e30
FORCE_BONUS = 1e6

LRU_WIDTH = D_MODEL // 2
LRU_HEADS = 16
LRU_HEAD_DIM = LRU_WIDTH // LRU_HEADS
CONV_WIDTH = 4
LRU_C = 8.0
POOL_WIDTH = D_MODEL // 2
POOL_WINDOWS = (2, 4, 8, 16)
POOL_GROUPS = len(POOL_WINDOWS)
POOL_GROUP_DIM = POOL_WIDTH // POOL_GROUPS
AB_IN = 2 * LRU_WIDTH + POOL_WIDTH
AB_MIX = LRU_WIDTH + POOL_WIDTH

N_HEADS = 16
HEAD_DIM = D_MODEL // N_HEADS
N_KV = 4
HPG = N_HEADS // N_KV
KV_WIDTH = N_KV * HEAD_DIM
CMP_LEN = 32
CMP_STRIDE = 16
CMP_HIDDEN = 512
SEL_LEN = 64
SEL_TOPN = 16
WINDOW = 512
N_BRANCH = 3
Q_BLOCK = 32
C_MIX = N_HEADS * HEAD_DIM
C_IN = C_MIX + 6 * KV_WIDTH + N_BRANCH * N_HEADS

N_EVEN = (DEPTH + 1) // 2
N_ODD = DEPTH // 2

kernel_name = 'hybrid_rglru_pool_nsa'


def rmsnorm(x, g):
    xf = x.astype(jnp.float32)
    y = xf * lax.rsqrt(jnp.mean(xf * xf, axis=-1, keepdims=True) + EPS)
    return (y * g.astype(jnp.float32)).astype(x.dtype)


def alibi_slopes():
    h = jnp.arange(1, N_HEADS + 1, dtype=jnp.float32)
    return jnp.exp2(-8.0 * h / N_HEADS).reshape(N_KV, HPG)


def causal_conv(x, w, b):
    S = x.shape[1]
    xp = jnp.pad(x, ((0, 0), (CONV_WIDTH - 1, 0), (0, 0)))
    out = b
    for k in range(CONV_WIDTH):
        out = out + xp[:, k:k + S] * w[k]
    return out


def _lin_combine(c1, c2):
    a1, b1 = c1
    a2, b2 = c2
    return a1 * a2, a2 * b1 + b2


def rg_lru(x, w_r, b_r, w_i, b_i, lam):
    B, S, W = x.shape
    xf = x.astype(jnp.float32)
    xh = xf.reshape(B, S, LRU_HEADS, LRU_HEAD_DIM)
    r = jax.nn.sigmoid(jnp.einsum('bshi,hij->bshj', xh, w_r.astype(jnp.float32)).reshape(B, S, W) + b_r.astype(jnp.float32))
    i = jax.nn.sigmoid(jnp.einsum('bshi,hij->bshj', xh, w_i.astype(jnp.float32)).reshape(B, S, W) + b_i.astype(jnp.float32))
    log_a = -LRU_C * r * jax.nn.softplus(-lam.astype(jnp.float32))
    a = jnp.exp(log_a)
    bvals = jnp.sqrt(-jnp.expm1(2.0 * log_a)) * (i * xf)
    _, h = lax.associative_scan(_lin_combine, (a, bvals), axis=1)
    return h.astype(x.dtype)


def pool_mixer(u, w, scale):
    B, S, _ = u.shape
    ug = u.astype(jnp.float32).reshape(B, S, POOL_GROUPS, POOL_GROUP_DIM)
    cs = jnp.cumsum(ug, axis=1)
    t = jnp.arange(S)
    outs = []
    for g, win in enumerate(POOL_WINDOWS):
        c = cs[:, :, g]
        prev = jnp.pad(c, ((0, 0), (win, 0), (0, 0)))[:, :S]
        cnt = jnp.minimum(t + 1, win).astype(jnp.float32)[None, :, None]
        outs.append((c - prev) / cnt - ug[:, :, g])
    d = jnp.stack(outs, axis=2)
    y = jnp.einsum('bsgi,gij->bsgj', d, w.astype(jnp.float32)).reshape(B, S, POOL_WIDTH)
    return (y * scale.astype(jnp.float32)).astype(u.dtype)


def ab_mixer(hn, w_in, conv_w, conv_b, w_r, b_r, w_i, b_i, lam, pool_w, pool_scale, w_out):
    z = hn @ w_in
    xr, gate, u = jnp.split(z, [LRU_WIDTH, 2 * LRU_WIDTH], axis=-1)
    xr = causal_conv(xr, conv_w, conv_b)
    y_lru = rg_lru(xr, w_r, b_r, w_i, b_i, lam) * jax.nn.gelu(gate)
    y_pool = pool_mixer(u, pool_w, pool_scale)
    return jnp.concatenate([y_lru, y_pool], axis=-1) @ w_out


def compress(kv, pos, w1, w2):
    S = kv.shape[1]
    n_cmp = (S - CMP_LEN) // CMP_STRIDE + 1
    idx = jnp.arange(n_cmp)[:, None] * CMP_STRIDE + jnp.arange(CMP_LEN)[None, :]
    blocks = kv[:, idx] + pos[None, None, :, None, :]
    hid = jax.nn.gelu(jnp.einsum('bnlgd,ldf->bngf', blocks, w1))
    return jnp.einsum('bngf,fd->bngd', hid, w2)


def nsa_mixer(hn, w_in, pos_k, w1_k, w2_k, pos_v, w1_v, w2_v, w_out):
    B, S, _ = hn.shape
    z = hn @ w_in
    splits = np.cumsum([C_MIX] + [KV_WIDTH] * 6).tolist()
    q, kc, vc, ks, vs, kw, vw, gl = jnp.split(z, splits, axis=-1)
    q = q.reshape(B, S, N_KV, HPG, HEAD_DIM) * (HEAD_DIM ** -0.5)
    kv_shape = (B, S, N_KV, HEAD_DIM)
    kc, vc, ks, vs, kw, vw = [a.reshape(kv_shape) for a in (kc, vc, ks, vs, kw, vw)]
    gates = jax.nn.sigmoid(gl.astype(jnp.float32)).reshape(B, S, N_BRANCH, N_KV, HPG)

    k_cmp = compress(kc, pos_k, w1_k, w2_k).astype(jnp.float32)
    v_cmp = compress(vc, pos_v, w1_v, w2_v).astype(jnp.float32)
    n_cmp = k_cmp.shape[1]
    n_sel = S // SEL_LEN
    top_n = min(SEL_TOPN, n_sel)
    cmp_start = jnp.arange(n_cmp) * CMP_STRIDE
    cmp_end = cmp_start + CMP_LEN - 1
    sel_start = jnp.arange(n_sel) * SEL_LEN
    ov = ((cmp_start[:, None] < sel_start[None, :] + SEL_LEN) & (cmp_start[:, None] + CMP_LEN > sel_start[None, :])).astype(jnp.float32)
    ks_blocks = ks.reshape(B, n_sel, SEL_LEN, N_KV, HEAD_DIM).transpose(0, 3, 1, 2, 4)
    vs_blocks = vs.reshape(B, n_sel, SEL_LEN, N_KV, HEAD_DIM).transpose(0, 3, 1, 2, 4)
    kw_pad = jnp.pad(kw, ((0, 0), (WINDOW, 0), (0, 0), (0, 0)))
    vw_pad = jnp.pad(vw, ((0, 0), (WINDOW, 0), (0, 0), (0, 0)))
    slopes = alibi_slopes()
    n_qblk = S // Q_BLOCK
    q_blocks = q.reshape(B, n_qblk, Q_BLOCK, N_KV, HPG, HEAD_DIM).swapaxes(0, 1)
    g_blocks = gates.reshape(B, n_qblk, Q_BLOCK, N_BRANCH, N_KV, HPG).swapaxes(0, 1)
    bi = jnp.arange(B)[:, None, None, None]
    gi = jnp.arange(N_KV)[None, :, None, None]
    jj = jnp.arange(n_sel)

    def one_block(args):
        blk, qb, gb = args
        t0 = blk * Q_BLOCK
        t = t0 + jnp.arange(Q_BLOCK)
        qf = qb.astype(jnp.float32)
        dist_c = (t[:, None] - cmp_end[None, :]).astype(jnp.float32)
        valid_c = dist_c >= 0
        s_c = jnp.einsum('bqghd,bngd->bghqn', qf, k_cmp) - slopes[None, :, :, None, None] * dist_c
        s_c = jnp.where(valid_c, s_c, NEG)
        p_c = jnp.where(valid_c, jax.nn.softmax(s_c, axis=-1), 0.0)
        o_c = jnp.einsum('bghqn,bngd->bqghd', p_c, v_cmp)
        imp = jnp.einsum('bghqn,nm->bgqm', p_c, ov)
        cur = t // SEL_LEN
        forced = (jj[None, :] == 0) | (jj[None, :] == cur[:, None]) | (jj[None, :] == cur[:, None] - 1)
        causal_s = sel_start[None, :] <= t[:, None]
        imp = jnp.where(forced, imp + FORCE_BONUS, imp)
        imp = jnp.where(causal_s, imp, NEG)
        _, idx = lax.top_k(imp, top_n)
        k_g = ks_blocks[bi, gi, idx].astype(jnp.float32)
        v_g = vs_blocks[bi, gi, idx].astype(jnp.float32)
        pos = idx[..., None] * SEL_LEN + jnp.arange(SEL_LEN)
        dist_s = (t[None, None, :, None, None] - pos).astype(jnp.float32)[:, :, None]
        s_s = jnp.einsum('bqghd,bgqnkd->bghqnk', qf, k_g) - slopes[None, :, :, None, None, None] * dist_s
        s_s = jnp.where(dist_s >= 0, s_s, NEG).reshape(B, N_KV, HPG, Q_BLOCK, top_n * SEL_LEN)
        p_s = jax.nn.softmax(s_s, axis=-1).reshape(B, N_KV, HPG, Q_BLOCK, top_n, SEL_LEN)
        o_s = jnp.einsum('bghqnk,bgqnkd->bqghd', p_s, v_g)
        kwb = lax.dynamic_slice_in_dim(kw_pad, t0, WINDOW + Q_BLOCK, axis=1).astype(jnp.float32)
        vwb = lax.dynamic_slice_in_dim(vw_pad, t0, WINDOW + Q_BLOCK, axis=1).astype(jnp.float32)
        s_pos = t0 - WINDOW + jnp.arange(WINDOW + Q_BLOCK)
        dist_w = t[:, None] - s_pos[None, :]
        valid_w = (dist_w >= 0) & (dist_w < WINDOW) & (s_pos[None, :] >= 0)
        s_w = jnp.einsum('bqghd,bkgd->bghqk', qf, kwb) - slopes[None, :, :, None, None] * dist_w.astype(jnp.float32)
        p_w = jax.nn.softmax(jnp.where(valid_w, s_w, NEG), axis=-1)
        o_w = jnp.einsum('bghqk,bkgd->bqghd', p_w, vwb)
        o = gb[:, :, 0, :, :, None] * o_c + gb[:, :, 1, :, :, None] * o_s + gb[:, :, 2, :, :, None] * o_w
        return o.astype(hn.dtype)

    o = lax.map(one_block, (jnp.arange(n_qblk), q_blocks, g_blocks))
    o = o.swapaxes(0, 1).reshape(B, S, C_MIX)
    return o @ w_out


def setup_inputs(seed: int = 0) -> dict:
    key = jax.random.key(seed)
    keys = iter(jax.random.split(key, 48))

    def nrm(shape, scale):
        return jax.random.normal(next(keys), shape, jnp.float32) * scale

    def gain(shape):
        return 1.0 + nrm(shape, 0.05)

    u = jax.random.uniform(next(keys), (N_EVEN, LRU_WIDTH), jnp.float32, 0.9, 0.999)
    s = u ** (1.0 / LRU_C)
    lam = jnp.log(s) - jnp.log1p(-s)
    return {
        'x': nrm((BATCH, SEQ, D_MODEL), 1.0),
        'p': nrm((DEPTH, BATCH, SEQ, PLE_DIM), 1.0),
        'ln_mix_g': gain((DEPTH, D_MODEL)),
        'ab_w_in': nrm((N_EVEN, D_MODEL, AB_IN), D_MODEL ** -0.5),
        'ab_conv_w': nrm((N_EVEN, CONV_WIDTH, LRU_WIDTH), CONV_WIDTH ** -0.5),
        'ab_conv_b': nrm((N_EVEN, LRU_WIDTH), 0.01),
        'ab_w_rgate': nrm((N_EVEN, LRU_HEADS, LRU_HEAD_DIM, LRU_HEAD_DIM), LRU_HEAD_DIM ** -0.5),
        'ab_b_rgate': nrm((N_EVEN, LRU_WIDTH), 0.1),
        'ab_w_igate': nrm((N_EVEN, LRU_HEADS, LRU_HEAD_DIM, LRU_HEAD_DIM), LRU_HEAD_DIM ** -0.5),
        'ab_b_igate': nrm((N_EVEN, LRU_WIDTH), 0.1),
        'ab_lambda': lam,
        'ab_pool_w': nrm((N_EVEN, POOL_GROUPS, POOL_GROUP_DIM, POOL_GROUP_DIM), POOL_GROUP_DIM ** -0.5),
        'ab_pool_scale': gain((N_EVEN, POOL_WIDTH)),
        'ab_w_out': nrm((N_EVEN, AB_MIX, D_MODEL), AB_MIX ** -0.5),
        'c_w_in': nrm((N_ODD, D_MODEL, C_IN), D_MODEL ** -0.5),
        'c_cmp_pos_k': nrm((N_ODD, CMP_LEN, HEAD_DIM), 0.02),
        'c_cmp_w1_k': nrm((N_ODD, CMP_LEN, HEAD_DIM, CMP_HIDDEN), (CMP_LEN * HEAD_DIM) ** -0.5),
        'c_cmp_w2_k': nrm((N_ODD, CMP_HIDDEN, HEAD_DIM), CMP_HIDDEN ** -0.5),
        'c_cmp_pos_v': nrm((N_ODD, CMP_LEN, HEAD_DIM), 0.02),
        'c_cmp_w1_v': nrm((N_ODD, CMP_LEN, HEAD_DIM, CMP_HIDDEN), (CMP_LEN * HEAD_DIM) ** -0.5),
        'c_cmp_w2_v': nrm((N_ODD, CMP_HIDDEN, HEAD_DIM), CMP_HIDDEN ** -0.5),
        'c_w_out': nrm((N_ODD, C_MIX, D_MODEL), C_MIX ** -0.5),
        'ln_mlp_g': gain((DEPTH, D_MODEL)),
        'mlp_w_up': nrm((DEPTH, D_MODEL, D_FF), D_MODEL ** -0.5),
        'mlp_w_down': nrm((DEPTH, D_FF, D_MODEL), D_FF ** -0.5),
        'ln_ple_g': gain((DEPTH, D_MODEL)),
        'ple_w_gate': nrm((DEPTH, D_MODEL, D_MODEL), D_MODEL ** -0.5),
        'ple_w_proj': nrm((DEPTH, PLE_DIM, D_MODEL), PLE_DIM ** -0.5),
        'ln_final_g': gain((D_MODEL,)),
    }


def reference(x, p, ln_mix_g, ab_w_in, ab_conv_w, ab_conv_b, ab_w_rgate, ab_b_rgate, ab_w_igate, ab_b_igate, ab_lambda, ab_pool_w, ab_pool_scale, ab_w_out, c_w_in, c_cmp_pos_k, c_cmp_w1_k, c_cmp_w2_k, c_cmp_pos_v, c_cmp_w1_v, c_cmp_w2_v, c_w_out, ln_mlp_g, mlp_w_up, mlp_w_down, ln_ple_g, ple_w_gate, ple_w_proj, ln_final_g):
    h = x
    for i in range(DEPTH):
        j = i // 2
        hn = rmsnorm(h, ln_mix_g[i])
        if i % 2 == 0:
            h = h + ab_mixer(hn, ab_w_in[j], ab_conv_w[j], ab_conv_b[j], ab_w_rgate[j], ab_b_rgate[j], ab_w_igate[j], ab_b_igate[j], ab_lambda[j], ab_pool_w[j], ab_pool_scale[j], ab_w_out[j])
        else:
            h = h + nsa_mixer(hn, c_w_in[j], c_cmp_pos_k[j], c_cmp_w1_k[j], c_cmp_w2_k[j], c_cmp_pos_v[j], c_cmp_w1_v[j], c_cmp_w2_v[j], c_w_out[j])
        hn = rmsnorm(h, ln_mlp_g[i])
        h = h + jnp.square(jax.nn.relu(hn @ mlp_w_up[i])) @ mlp_w_down[i]
        gate = jax.nn.sigmoid(rmsnorm(h, ln_ple_g[i]) @ ple_w_gate[i])
        h = h + gate * (p[i] @ ple_w_proj[i])
    return rmsnorm(h, ln_final_g)
```

```python
import numpy as np
import ml_dtypes
import concourse.bass as bass
import concourse.mybir as mybir
from concourse.bass_utils import run_bass_kernel_spmd

F32 = mybir.dt.float32
BF16 = mybir.dt.bfloat16
AF = mybir.ActivationFunctionType
ALU = mybir.AluOpType

D = 2048
B = 4
S = 4096
DEPTH = 4
DFF = 8192
PLE = 256
NCORE = 8
TPC = 2048
TT = 512
EPS = 1e-6
AB_IN = 3072
C_IN = 5168

ENGS = ("sync", "act", "pool", "pe", "dve")


class Op:
    __slots__ = ("eng", "fn", "deps", "sig", "count", "slot")

    def __init__(self, eng, fn, deps, slot=None):
        self.eng = eng
        self.fn = fn
        self.deps = deps
        self.sig = slot is not None
        self.count = 0
        self.slot = slot


class Slot:
    def __init__(self, name):
        self.name = name
        self.n = 0
        self.sem = None


class Prog:
    def __init__(self, nc):
        self.nc = nc
        self.ops = {e: [] for e in ENGS}
        self.slots = []

    def slot(self, name):
        s = Slot(name)
        self.slots.append(s)
        return s

    def op(self, eng, fn, deps=()):
        o = Op(eng, fn, [d for d in deps if d is not None])
        self.ops[eng].append(o)
        return o

    def dma(self, eng, slot, fn, deps=()):
        o = Op(eng, fn, [d for d in deps if d is not None], slot=slot)
        slot.n += 16
        o.count = slot.n
        self.ops[eng].append(o)
        return o

    def finalize(self):
        nc = self.nc
        for e in ENGS:
            for o in self.ops[e]:
                for d in o.deps:
                    d.sig = True
        for e in ENGS:
            c = 0
            for o in self.ops[e]:
                if o.slot is None and o.sig:
                    c += 1
                    o.count = c
        import contextlib
        with contextlib.ExitStack() as st:
            esem = {e: st.enter_context(nc.semaphore("s_" + e)) for e in ENGS}
            for s in self.slots:
                s.sem = st.enter_context(nc.semaphore("d_" + s.name))
            block = st.enter_context(nc.Block())

            def replay(ename, eh):
                waited = {}
                for o in self.ops[ename]:
                    need = {}
                    for d in o.deps:
                        if d.slot is not None:
                            key, sem = d.slot.name, d.slot.sem
                        else:
                            if d.eng == "pe" and ename == "pe":
                                continue
                            key, sem = d.eng, esem[d.eng]
                        if waited.get(key, 0) < d.count and need.get(key, (0, None))[0] < d.count:
                            need[key] = (d.count, sem)
                    for key, (cnt, sem) in need.items():
                        eh.wait_ge(sem, cnt)
                        waited[key] = cnt
                    ins = o.fn(eh)
                    if o.slot is not None:
                        ins.then_inc(o.slot.sem, 16)
                    elif o.sig:
                        ins.then_inc(esem[ename], 1)

            @block.sync
            def _(eh):
                replay("sync", eh)

            @block.scalar
            def _(eh):
                replay("act", eh)

            @block.gpsimd
            def _(eh):
                replay("pool", eh)

            @block.tensor
            def _(eh):
                replay("pe", eh)

            @block.vector
            def _(eh):
                replay("dve", eh)


class Banks:
    def __init__(self, ps, n=8):
        self.ps = ps
        self.n = n
        self.free = [None] * n
        self.i = 0

    def get(self):
        b = self.i
        self.i = (self.i + 1) % self.n
        return b, self.free[b]

    def release(self, b, op):
        self.free[b] = op


class WRing:
    def __init__(self, P, tiles):
        self.P = P
        self.tiles = tiles
        self.n = len(tiles)
        self.slots = [P.slot("w%d" % i) for i in range(self.n)]
        self.users = [[] for _ in range(self.n)]
        self.i = 0

    def load(self, fn_list):
        i = self.i
        self.i = (self.i + 1) % self.n
        t = self.tiles[i]
        deps = self.users[i]
        self.users[i] = []
        tok = None
        for k, fn in enumerate(fn_list):
            tok = self.P.dma("pool", self.slots[i], (lambda eh, fn=fn, t=t: fn(eh, t)), deps if k == 0 else ())
        return i, t, tok

    def used_by(self, i, op):
        self.users[i].append(op)


def _act(P, out, in_, func, deps, bias=None, scale=None):
    kw = {}
    if bias is not None:
        kw["bias"] = bias
    if scale is not None:
        kw["scale"] = scale
    return P.op("act", lambda e: e.activation(out=out, in_=in_, func=func, **kw), deps)


def build_PA(lp, li, cin):
    nc = bass.Bass("TRN2", target_bir_lowering=False)
    first = lp is None
    last = li is None
    dr = {}

    def din(name, shape, dt=F32):
        dr[name] = nc.dram_tensor(name, shape, dt, kind="ExternalInput").ap()
        return dr[name]

    def dout(name, shape, dt=F32):
        dr[name] = nc.dram_tensor(name, shape, dt, kind="ExternalOutput").ap()
        return dr[name]

    gv = din("gv", [128, 4, 16])
    ident_d = din("ident", [128, 128])
    if first:
        x_tok = din("x_tok", [TPC, D])
    else:
        hT_in = din("hT_in", [D, TPC])
        yT = din("yT", [D, TPC], BF16)
        p_tok = din("p_tok", [TPC, PLE])
        w_out = din("w_out", [D, D])
        w_up = din("w_up", [D, DFF])
        w_down = din("w_down", [DFF, D])
        w_gate = din("w_gate", [D, D])
        w_proj = din("w_proj", [PLE, D])
    if not last:
        w_in = din("w_in", [D, cin])
        zT = dout("zT", [cin, TPC])
        hT_out = dout("hT_out", [D, TPC])
    else:
        out_tok = dout("out_tok", [TPC, D])

    import contextlib
    with contextlib.ExitStack() as st:
        def sb(name, shape, dt):
            return st.enter_context(nc.sbuf_tensor(name, shape, dt))

        H = sb("H", [128, 16, TT], F32)
        HN = sb("HN", [128, 16, TT], BF16)
        A = sb("A", [128, 64, TT], BF16)
        NW = 3
        WT = [sb("W%d" % i, [128, 8192], BF16) for i in range(NW)]
        GV = sb("GV", [128, 4, 16], F32)
        IDF = sb("IDF", [128, 128], F32)
        ONES = sb("ONES", [128, 128], BF16)
        RS = sb("RS", [128, TT], F32)
        RSTD = sb("RSTD", [128, TT], F32)
        TMP = [sb("TMP%d" % i, [128, TT], F32) for i in range(3)]
        ZST = [sb("ZST%d" % i, [128, TT], F32) for i in range(3)]
        PT = sb("PT", [128, 2, TT], BF16)
        PST = sb("PST", [128, 4, PLE], F32)
        PJ = sb("PJ", [128, 2, D], BF16)
        ps = st.enter_context(nc.psum_tensor("ps", [128, 8, TT], F32))

        SQ = A[:, 0:16, :]
        YT = A[:, 16:32, :]
        XST = A[:, 32:64, :].bitcast(F32).rearrange("p a b -> p (a b)").rearrange("p (s d) -> p s d", d=D)

        P = Prog(nc)
        banks = Banks(ps)
        wr = WRing(P, WT)
        s_c = P.slot("const")
        s_h = P.slot("hload")
        s_y = P.slot("yload")
        s_p = P.slot("pload")
        s_z = [P.slot("zst%d" % i) for i in range(3)]
        s_ho = P.slot("hstore")
        s_o = P.slot("ostore")

        c0 = P.dma("sync", s_c, lambda e: e.dma_start(out=GV[:], in_=gv))
        c1 = P.dma("sync", s_c, lambda e: e.dma_start(out=IDF[:], in_=ident_d))
        c2 = P.op("dve", lambda e: e.memset(ONES[:], 1.0))
        consts = [c1, c2]
        pj_load = None
        if not first:
            s_pj = P.slot("pj")
            pj_load = P.dma("pool", s_pj, lambda e: e.dma_start(out=PJ[:], in_=w_proj.rearrange("(k p) n -> p k n", p=128)))

        h_writers = []
        a_readers = []
        hn_readers = []
        zst_i = [0]
        zst_free = [None, None, None]
        tmp_i = [0]
        tmp_free = [None, None, None]

        def evac_copy(k, out, in_, deps):
            if k % 2 == 0:
                return P.op("act", lambda e: e.copy(out=out, in_=in_), deps)
            return P.op("dve", lambda e: e.tensor_copy(out=out, in_=in_), deps)

        def rmsnorm(gi, deps_h, deps_sq_free, deps_hn_free, out_fp32=None):
            sq_ops = []
            for q in range(4):
                sq_ops.append(_act(P, SQ[:, 4 * q:4 * q + 4, :], H[:, 4 * q:4 * q + 4, :], AF.Square,
                                   list(deps_h) + list(deps_sq_free)))
            b, fr = banks.get()
            mm = None
            for c in range(16):
                mm = P.op("pe", lambda e, c=c, b=b: e.matmul(ps[:, b, :], ONES[:], SQ[:, c, :], start=(c == 0), stop=(c == 15)),
                          [sq_ops[c // 4], fr, c2] if c % 4 == 0 else [])
            o1 = _act(P, RS[:], ps[:, b, :], AF.Sqrt, [mm, ce], bias=EPS_T[:, 0:1], scale=1.0 / D)
            banks.release(b, o1)
            o2 = P.op("dve", lambda e: e.reciprocal(out=RSTD[:], in_=RS[:]), [o1])
            outs = []
            for c in range(16):
                if out_fp32 is None:
                    dst = HN[:, c, :]
                else:
                    dst = out_fp32[:, c, :]
                outs.append(P.op("dve", lambda e, c=c, dst=dst: e.scalar_tensor_tensor(
                    out=dst, in0=H[:, c, :], scalar=GV[:, gi, c:c + 1], in1=RSTD[:], op0=ALU.mult, op1=ALU.mult),
                    [o2, c1] + (list(deps_hn_free) if c == 0 else [])))
            return outs, mm

        EPS_T = sb("EPS_T", [128, 1], F32)
        ce = P.op("dve", lambda e: e.memset(EPS_T[:], EPS))

        def gemm(w_ap, ncols, nk, rhs_fn, rhs_deps, evac, wrows_fn=None, colblk=512):
            readers = []
            if nk * colblk > 8192:
                colblk = 8192 // nk
            for c0_ in range(0, ncols, colblk):
                cw = min(colblk, ncols - c0_)
                wv = w_ap.rearrange("(k p) n -> p k n", p=128)

                def ld(eh, t, c0_=c0_, cw=cw):
                    tv = t[:, 0:nk * cw].rearrange("p (k n) -> p k n", n=cw)
                    return eh.dma_start(out=tv, in_=wv[:, :, c0_:c0_ + cw])
                wi, wt, wtok = wr.load([ld])
                tv = wt[:, 0:nk * cw].rearrange("p (k n) -> p k n", n=cw)
                for m0 in range(0, cw, 128):
                    mw = min(128, cw - m0)
                    b, fr = banks.get()
                    mm = None
                    for k in range(nk):
                        mm = P.op("pe", lambda e, k=k, b=b, m0=m0, mw=mw, tv=tv: e.matmul(
                            ps[0:mw, b, :], tv[:, k, m0:m0 + mw], rhs_fn(k), start=(k == 0), stop=(k == nk - 1)),
                            ([wtok, fr] + list(rhs_deps)) if k == 0 else [])
                    wr.used_by(wi, mm)
                    readers.append(mm)
                    ev = evac((c0_ + m0) // 128, b, mm, mw)
                    banks.release(b, ev)
            return readers

        ntile = TPC // TT
        prev_tile_done = []
        for tt in range(ntile):
            tsl = slice(tt * TT, (tt + 1) * TT)
            if first:
                xv = x_tok[tsl, :].rearrange("(s p) d -> p s d", p=128)
                ld = P.dma("sync", s_h, lambda e, xv=xv: e.dma_start(out=XST, in_=xv), a_readers + prev_tile_done)
                a_readers = []
                hw = []
                for c in range(16):
                    b, fr = banks.get()
                    tp = None
                    for s_ in range(4):
                        tp = P.op("pe", lambda e, c=c, s_=s_, b=b: e.transpose(
                            out=ps[:, b, s_ * 128:(s_ + 1) * 128], in_=XST[:, s_, c * 128:(c + 1) * 128], identity=IDF[:]),
                            [ld, fr, c1] + prev_tile_done if s_ == 0 else [])
                    ev = evac_copy(c, H[:, c, :], ps[:, b, :], [tp] + prev_tile_done)
                    banks.release(b, ev)
                    hw.append(ev)
                    a_readers.append(tp)
                h_ready = hw
            else:
                ld = P.dma("sync", s_h, lambda e, tsl=tsl: e.dma_start(
                    out=H[:], in_=hT_in.rearrange("(c p) t -> p c t", p=128)[:, :, tsl]), prev_tile_done)
                ldy = P.dma("sync", s_y, lambda e, tsl=tsl: e.dma_start(
                    out=YT, in_=yT.rearrange("(c p) t -> p c t", p=128)[:, :, tsl]), a_readers + prev_tile_done)
                a_readers = []
                ldp = P.dma("sync", s_p, lambda e, tsl=tsl: e.dma_start(
                    out=PST[:], in_=p_tok[tsl, :].rearrange("(s p) d -> p s d", p=128)), prev_tile_done)
                pt_ops = []
                for kc in range(2):
                    b, fr = banks.get()
                    tp = None
                    for s_ in range(4):
                        tp = P.op("pe", lambda e, kc=kc, s_=s_, b=b: e.transpose(
                            out=ps[:, b, s_ * 128:(s_ + 1) * 128], in_=PST[:, s_, kc * 128:(kc + 1) * 128], identity=IDF[:]),
                            [ldp, fr, c1] + prev_tile_done if s_ == 0 else [])
                    ev = evac_copy(kc, PT[:, kc, :], ps[:, b, :], [tp] + prev_tile_done)
                    banks.release(b, ev)
                    pt_ops.append(ev)
                hw = []

                def ev_add(m, b, mm, mw):
                    o = P.op("dve", lambda e, m=m, b=b: e.tensor_tensor(out=H[:, m, :], in0=ps[:, b, :], in1=H[:, m, :], op=ALU.add), [mm, ld])
                    hw.append(o)
                    return o
                rd = gemm(w_out, D, 16, lambda k: YT[:, k, :], [ldy], ev_add)
                a_readers += rd
                hn_ops, _ = rmsnorm(0, hw, a_readers, hn_readers)
                hn_readers = []
                a_readers = []
                a_w = []

                def ev_up(m, b, mm, mw):
                    ti = tmp_i[0]
                    tmp_i[0] = (ti + 1) % 3
                    o1 = _act(P, TMP[ti][:], ps[:, b, :], AF.Square, [mm, tmp_free[ti]])
                    o2 = P.op("dve", lambda e, m=m, b=b, ti=ti: e.scalar_tensor_tensor(
                        out=A[:, m, :], in0=ps[:, b, :], scalar=0.0, in1=TMP[ti][:], op0=ALU.is_gt, op1=ALU.mult), [o1, mm] + (hn_sq_done if m < 16 else []) + (a_free if m >= 16 else []))
                    tmp_free[ti] = o2
                    a_w.append(o2)
                    return o2
                hn_sq_done = []
                a_free = list(rd)
                rd_up = gemm(w_up, DFF, 16, lambda k: HN[:, k, :], hn_ops, ev_up)
                hn_readers += rd_up
                hw2 = []

                def ev_add2(m, b, mm, mw):
                    o = P.op("dve", lambda e, m=m, b=b: e.tensor_tensor(out=H[:, m, :], in0=ps[:, b, :], in1=H[:, m, :], op=ALU.add), [mm] + hn_ops)
                    hw2.append(o)
                    return o
                rd_dn = gemm(w_down, D, 64, lambda k: A[:, k, :], a_w, ev_add2, colblk=128)
                a_readers += rd_dn
                hn2_ops, _ = rmsnorm(1, hw2, a_readers, hn_readers)
                hn_readers = []
                a_readers = []
                hw3 = []
                pjv = PJ[:]
                pjtok = pj_load

                def ev_gate(m, b, mm, mw):
                    ti = tmp_i[0]
                    tmp_i[0] = (ti + 1) % 3
                    o1 = _act(P, TMP[ti][:], ps[:, b, :], AF.Sigmoid, [mm, tmp_free[ti]])
                    b2, fr2 = banks.get()
                    m2 = None
                    for k in range(2):
                        m2 = P.op("pe", lambda e, k=k, b2=b2, m=m: e.matmul(
                            ps[:, b2, :], pjv[:, k, m * 128:(m + 1) * 128], PT[:, k, :], start=(k == 0), stop=(k == 1)),
                            [pjtok, fr2] + pt_ops if k == 0 else [])
                    o2 = P.op("dve", lambda e, ti=ti, b2=b2: e.tensor_tensor(out=TMP[ti][:], in0=ps[:, b2, :], in1=TMP[ti][:], op=ALU.mult), [o1, m2])
                    banks.release(b2, o2)
                    o3 = P.op("dve", lambda e, ti=ti, m=m: e.tensor_tensor(out=H[:, m, :], in0=TMP[ti][:], in1=H[:, m, :], op=ALU.add), [o2] + hn2_ops)
                    tmp_free[ti] = o3
                    hw3.append(o3)
                    return o1
                rd_g = gemm(w_gate, D, 16, lambda k: HN[:, k, :], hn2_ops, ev_gate)
                hn_readers += rd_g
                h_ready = hw3

            if not last:
                st_h = P.dma("sync", s_ho, lambda e, tsl=tsl: e.dma_start(
                    out=hT_out.rearrange("(c p) t -> p c t", p=128)[:, :, tsl], in_=H[:]), h_ready)
                hn3_ops, _ = rmsnorm(2, h_ready, a_readers, hn_readers)
                hn_readers = []
                a_readers = []
                z_st = []

                def ev_z(m, b, mm, mw):
                    zi = zst_i[0]
                    zst_i[0] = (zi + 1) % 3
                    o1 = evac_copy(m, ZST[zi][0:mw, :], ps[0:mw, b, :], [mm, zst_free[zi]])
                    o2 = P.dma("sync", s_z[zi], lambda e, m=m, zi=zi, mw=mw, tsl=tsl: e.dma_start(
                        out=zT[m * 128:m * 128 + mw, tsl], in_=ZST[zi][0:mw, :]), [o1])
                    zst_free[zi] = o2
                    z_st.append(o2)
                    return o1
                rd_in = gemm(w_in, cin, 16, lambda k: HN[:, k, :], hn3_ops, ev_z)
                hn_readers += rd_in
                prev_tile_done = [st_h] + hn3_ops
                final_waits = z_st[-3:] + [st_h]
            else:
                OUTF = A[:, 0:32, :].bitcast(F32).rearrange("p a b -> p (a b)").rearrange("p (c t) -> p c t", t=TT)
                outs, mmsq = rmsnorm(2, h_ready, a_readers, hn_readers, out_fp32=OUTF)
                OST = XST
                o_st = []
                evs = []
                for c in range(16):
                    b, fr = banks.get()
                    for s_ in range(4):
                        tp = P.op("pe", lambda e, c=c, s_=s_, b=b: e.transpose(
                            out=ps[:, b, s_ * 128:(s_ + 1) * 128], in_=OUTF[:, c, s_ * 128:(s_ + 1) * 128], identity=IDF[:]),
                            [outs[c], fr, c1])
                    ev = evac_copy(c, OST[:, :, c * 128:(c + 1) * 128], ps[:, b, :].rearrange("p (s d) -> p s d", d=128), [tp] + prev_tile_done)
                    banks.release(b, ev)
                    evs.append(ev)
                ov = out_tok[tsl, :].rearrange("(s p) d -> p s d", p=128)
                so = P.dma("sync", s_o, lambda e, ov=ov: e.dma_start(out=ov, in_=OST), evs)
                a_readers = [so]
                prev_tile_done = [so] + outs
                final_waits = [so]
        P.op("sync", lambda e: e.nop(), final_waits)
        if not last:
            P.op("sync", lambda e: e.nop(), [zst_free[i] for i in range(3)])
        P.finalize()
    return nc


def _gvec(*vs):
    out = np.zeros((128, 4, 16), np.float32)
    for i, v in enumerate(vs):
        out[:, i, :] = np.asarray(v, np.float32).reshape(16, 128).T
    return out


_NC_CACHE = {}


def _get_nc(key, builder):
    if key not in _NC_CACHE:
        _NC_CACHE[key] = builder()
    return _NC_CACHE[key]


def run_PA(lp, li, inp, hT_list, yT_list):
    cin = None if li is None else (AB_IN if li % 2 == 0 else C_IN)
    nc = _get_nc(("PA", lp is None, li is None, cin), lambda: build_PA(lp, li, cin))
    ident = np.eye(128, dtype=np.float32)
    in_maps = []
    for c in range(NCORE):
        b, hf = c // 2, c % 2
        rows = slice(hf * TPC, (hf + 1) * TPC)
        m = {"ident": ident}
        g2 = inp["ln_final_g"] if li is None else inp["ln_mix_g"][li]
        if lp is None:
            m["gv"] = _gvec(g2, g2, g2)
            m["x_tok"] = np.ascontiguousarray(inp["x"][b, rows])
        else:
            j = lp // 2
            m["gv"] = _gvec(inp["ln_mlp_g"][lp], inp["ln_ple_g"][lp], g2)
            m["hT_in"] = hT_list[c]
            m["yT"] = yT_list[c]
            m["p_tok"] = np.ascontiguousarray(inp["p"][lp, b, rows])
            m["w_out"] = inp["ab_w_out"][j] if lp % 2 == 0 else inp["c_w_out"][j]
            m["w_up"] = inp["mlp_w_up"][lp]
            m["w_down"] = inp["mlp_w_down"][lp]
            m["w_gate"] = inp["ple_w_gate"][lp]
            m["w_proj"] = inp["ple_w_proj"][lp]
        if li is not None:
            m["w_in"] = inp["ab_w_in"][li // 2] if li % 2 == 0 else inp["c_w_in"][li // 2]
        in_maps.append(m)
    res = run_bass_kernel_spmd(nc, in_maps, core_ids=list(range(NCORE)))
    return res.results


class TB:
    def __init__(self, ap):
        self.ap = ap
        self.w = []
        self.r = []


def emit(P, eng, fn, reads=(), writes=(), pwrites=(), extra=(), slot=None):
    deps = [d for d in extra if d is not None]
    for b in reads:
        deps += b.w
    for b in writes:
        deps += b.r
        deps += b.w
    for b in pwrites:
        deps += b.r
    if slot is None:
        o = P.op(eng, fn, deps)
    else:
        o = P.dma(eng, slot, fn, deps)
    for b in reads:
        b.r.append(o)
    for b in writes:
        b.w = [o]
        b.r = []
    for b in pwrites:
        b.w.append(o)
    return o


def begin_refill(b):
    b.r = b.r + b.w
    b.w = []


def end_refill(b):
    b.r = []


POOL_WINDOWS = (2, 4, 8, 16)


def build_ME():
    nc = bass.Bass("TRN2", target_bir_lowering=False)
    L = S
    PADL = 16

    def din(name, shape, dt=F32):
        return nc.dram_tensor(name, shape, dt, kind="ExternalInput").ap()

    xrT = din("xrT", [512, L])
    gT = din("gT", [512, L])
    uT = din("uT", [1024, L])
    vecs_d = din("vecs", [128, 4, 12])
    wbd_d = din("wbd", [128, 2, 4, 128])
    pw_d = din("pw", [128, 4, 2, 128])
    invc_d = din("invc", [128, 16])
    yT = nc.dram_tensor("yT", [1024, L], BF16, kind="ExternalOutput").ap()

    import contextlib
    with contextlib.ExitStack() as st:
        def sb(name, shape, dt):
            return st.enter_context(nc.sbuf_tensor(name, shape, dt))
        BUF = [sb("BUF%d" % i, [128, PADL + L], F32) for i in range(6)]
        XCB = sb("XCB", [128, L], BF16)
        DB = sb("DB", [128, 2, L], BF16)
        YB = [sb("YB%d" % i, [128, L], BF16) for i in range(2)]
        VEC = sb("VEC", [128, 4, 12], F32)
        C8 = sb("C8", [128, 4], F32)
        WBDF = sb("WBDF", [128, 2, 4, 128], F32)
        WBD = sb("WBD", [128, 2, 4, 128], BF16)
        PWF = sb("PWF", [128, 4, 2, 128], F32)
        PW = sb("PW", [128, 4, 2, 128], BF16)
        INVC = sb("INVC", [128, 16], F32)
        ONE_T = sb("ONE_T", [128, 1], F32)
        SM = sb("SM", [128, 16], F32)
        ps = st.enter_context(nc.psum_tensor("ps", [128, 8, 512], F32))

        P = Prog(nc)
        banks = Banks(ps)
        s_c = P.slot("const")
        s_l = [P.slot("ld%d" % i) for i in range(3)]
        s_st = [P.slot("st%d" % i) for i in range(2)]

        tb = {n: TB(None) for n in ["B0", "B1", "B2", "B3", "B4", "B5", "XCB", "DB", "YB0", "YB1", "SM"]}
        cons = []
        cons.append(P.dma("sync", s_c, lambda e: e.dma_start(out=VEC[:], in_=vecs_d)))
        cons.append(P.dma("sync", s_c, lambda e: e.dma_start(out=WBDF[:], in_=wbd_d)))
        cons.append(P.dma("sync", s_c, lambda e: e.dma_start(out=PWF[:], in_=pw_d)))
        cload = P.dma("sync", s_c, lambda e: e.dma_start(out=INVC[:], in_=invc_d))
        k1 = P.op("dve", lambda e: e.tensor_copy(out=WBD[:], in_=WBDF[:]), [cload])
        k2 = P.op("dve", lambda e: e.tensor_copy(out=PW[:], in_=PWF[:]), [cload])
        k3 = P.op("dve", lambda e: e.memset(ONE_T[:], 1.0))
        zs = [P.op("dve", lambda e, i=i: e.memset(BUF[i][:, 0:PADL], 0.0)) for i in range(3)]
        k4 = _act(P, C8[:], VEC[:, :, 7], AF.Exp, [cload], scale=-1.0)
        k5 = _act(P, C8[:], C8[:], AF.Ln, [k4, k3], bias=ONE_T[:, 0:1])
        k6 = P.op("dve", lambda e: e.tensor_scalar(out=C8[:], in0=C8[:], scalar1=-8.0, scalar2=None, op0=ALU.mult), [k5])
        cdeps = [cload, k1, k2, k3, k6] + zs

        XP = BUF[0]
        XC = BUF[1][:, PADL:]
        RG = BUF[2][:, PADL:]
        IG = BUF[3][:, PADL:]
        G = BUF[4][:, PADL:]
        T1 = BUF[5][:, PADL:]
        NTL = L // 512
        out_waits = {}

        for c in range(4):
            rows = slice(c * 128, (c + 1) * 128)
            emit(P, "sync", lambda e, rows=rows: e.dma_start(out=XP[:, PADL:], in_=xrT[rows, :]), writes=[tb["B0"]], slot=s_l[0])
            emit(P, "sync", lambda e, rows=rows: e.dma_start(out=G, in_=gT[rows, :]), writes=[tb["B4"]], slot=s_l[1])
            emit(P, "dve", lambda e, c=c: e.tensor_scalar(out=XC, in0=XP[:, PADL:], scalar1=VEC[:, c, 3:4], scalar2=VEC[:, c, 4:5],
                                                         op0=ALU.mult, op1=ALU.add), reads=[tb["B0"]], writes=[tb["B1"]], extra=cdeps)
            for k in range(3):
                emit(P, "dve", lambda e, c=c, k=k: e.scalar_tensor_tensor(out=XC, in0=XP[:, PADL - 3 + k:PADL - 3 + k + L], scalar=VEC[:, c, k:k + 1],
                                                                       in1=XC, op0=ALU.mult, op1=ALU.add), reads=[tb["B0"]], writes=[tb["B1"]])
            emit(P, "act", lambda e: e.copy(out=XCB[:], in_=XC), reads=[tb["B1"]], writes=[tb["XCB"]])
            begin_refill(tb["B2"])
            begin_refill(tb["B3"])
            for t in range(NTL):
                ts_ = slice(t * 512, (t + 1) * 512)
                for gi, (dst, tbn, bcol) in enumerate(((RG, "B2", 5), (IG, "B3", 6))):
                    b, fr = banks.get()
                    mm = emit(P, "pe", lambda e, b=b, gi=gi, c=c, ts_=ts_: e.matmul(ps[:, b, :], WBD[:, gi, c, :], XCB[:, ts_], start=True, stop=True),
                              reads=[tb["XCB"]], extra=[fr, k1])
                    ev = emit(P, "act", lambda e, b=b, dst=dst, ts_=ts_, c=c, bcol=bcol: e.activation(
                        out=dst[:, ts_], in_=ps[:, b, :], func=AF.Sigmoid, bias=VEC[:, c, bcol:bcol + 1]), pwrites=[tb[tbn]], extra=[mm])
                    banks.release(b, ev)
            end_refill(tb["B2"])
            end_refill(tb["B3"])
            emit(P, "act", lambda e, c=c: e.activation(out=RG, in_=RG, func=AF.Exp, scale=C8[:, c:c + 1]), reads=[tb["B2"]], writes=[tb["B2"]], extra=[k6])
            emit(P, "dve", lambda e: e.tensor_tensor(out=T1, in0=RG, in1=RG, op=ALU.mult), reads=[tb["B2"]], writes=[tb["B5"]])
            emit(P, "act", lambda e: e.activation(out=T1, in_=T1, func=AF.Sqrt, bias=ONE_T[:, 0:1], scale=-1.0), reads=[tb["B5"]], writes=[tb["B5"]])
            emit(P, "dve", lambda e: e.tensor_tensor(out=IG, in0=IG, in1=XC, op=ALU.mult), reads=[tb["B3"], tb["B1"]], writes=[tb["B3"]])
            emit(P, "dve", lambda e: e.tensor_tensor(out=IG, in0=IG, in1=T1, op=ALU.mult), reads=[tb["B3"], tb["B5"]], writes=[tb["B3"]])
            emit(P, "dve", lambda e: e.tensor_tensor_scan(out=XC, data0=RG, data1=IG, initial=0.0, op0=ALU.mult, op1=ALU.add),
                 reads=[tb["B2"], tb["B3"]], writes=[tb["B1"]])
            emit(P, "act", lambda e: e.activation(out=T1, in_=G, func=AF.Square), reads=[tb["B4"]], writes=[tb["B5"]])
            emit(P, "dve", lambda e: e.tensor_scalar(out=T1, in0=T1, scalar1=0.044715, scalar2=1.0, op0=ALU.mult, op1=ALU.add), reads=[tb["B5"]], writes=[tb["B5"]])
            emit(P, "dve", lambda e: e.tensor_tensor(out=T1, in0=T1, in1=G, op=ALU.mult), reads=[tb["B5"], tb["B4"]], writes=[tb["B5"]])
            emit(P, "act", lambda e: e.activation(out=T1, in_=T1, func=AF.Sigmoid, scale=1.5957691216057308), reads=[tb["B5"]], writes=[tb["B5"]])
            emit(P, "dve", lambda e: e.tensor_tensor(out=T1, in0=T1, in1=G, op=ALU.mult), reads=[tb["B5"], tb["B4"]], writes=[tb["B5"]])
            yi = c % 2
            emit(P, "dve", lambda e, yi=yi: e.tensor_tensor(out=YB[yi][:], in0=XC, in1=T1, op=ALU.mult), reads=[tb["B1"], tb["B5"]], writes=[tb["YB%d" % yi]])
            out_waits[yi] = emit(P, "sync", lambda e, yi=yi, rows=rows: e.dma_start(out=yT[rows, :], in_=YB[yi][:]), reads=[tb["YB%d" % yi]], slot=s_st[yi])

        UP = BUF[0]
        SA = BUF[1]
        SBb = BUF[2]
        ycount = 0
        for g in range(4):
            for ic in range(2):
                chn = g * 2 + ic
                rows = slice(chn * 128, (chn + 1) * 128)
                emit(P, "sync", lambda e, rows=rows: e.dma_start(out=UP[:, PADL:], in_=uT[rows, :]), writes=[tb["B0"]], slot=s_l[2])
                ME_pool_chunk(P, tb, UP, SA, SBb, DB, INVC, SM, ic, POOL_WINDOWS[g], L, PADL)
            yi = ycount % 2
            ycount += 1
            begin_refill(tb["YB%d" % yi])
            for t in range(NTL):
                ts_ = slice(t * 512, (t + 1) * 512)
                b, fr = banks.get()
                mm = None
                for ic in range(2):
                    mm = emit(P, "pe", lambda e, b=b, g=g, ic=ic, ts_=ts_: e.matmul(
                        ps[:, b, :], PW[:, g, ic, :], DB[:, ic, ts_], start=(ic == 0), stop=(ic == 1)),
                        reads=[tb["DB"]], extra=[fr, k2])
                ev = emit(P, "dve", lambda e, b=b, yi=yi, ts_=ts_, g=g: e.tensor_scalar(
                    out=YB[yi][:, ts_], in0=ps[:, b, :], scalar1=VEC[:, g, 8:9], scalar2=None, op0=ALU.mult),
                    pwrites=[tb["YB%d" % yi]], extra=[mm, cload])
                banks.release(b, ev)
            end_refill(tb["YB%d" % yi])
            orow = 512 + g * 128
            out_waits[yi] = emit(P, "sync", lambda e, yi=yi, orow=orow: e.dma_start(out=yT[orow:orow + 128, :], in_=YB[yi][:]),
                                 reads=[tb["YB%d" % yi]], slot=s_st[yi])
        P.op("sync", lambda e: e.nop(), list(out_waits.values()))
        P.finalize()
    return nc


def ME_pool_chunk(P, tb, UP, SA, SBb, DB, INVC, SM, ic, w, L, PADL):
    bufs = [(UP, "B0"), (SA, "B1"), (SBb, "B2")]
    src, srcn = UP, "B0"
    k = 0
    sh = 1
    while sh < w:
        dst, dstn = bufs[1 + (k % 2)]
        emit(P, "dve", lambda e, src=src, dst=dst, sh=sh: e.tensor_tensor(
            out=dst[:, PADL:], in0=src[:, PADL:], in1=src[:, PADL - sh:PADL - sh + L], op=ALU.add),
            reads=[tb[srcn]], writes=[tb[dstn]])
        src, srcn = dst, dstn
        k += 1
        sh *= 2
    emit(P, "dve", lambda e, src=src: e.scalar_tensor_tensor(
        out=DB[:, ic, :], in0=src[:, PADL:], scalar=1.0 / w, in1=UP[:, PADL:], op0=ALU.mult, op1=ALU.subtract),
        reads=[tb[srcn], tb["B0"]], pwrites=[tb["DB"]] if ic == 1 else [], writes=[tb["DB"]] if ic == 0 else [])
    emit(P, "dve", lambda e, src=src: e.tensor_tensor(out=SM[:, 0:w - 1], in0=src[:, PADL:PADL + w - 1], in1=INVC[:, 0:w - 1], op=ALU.mult),
         reads=[tb[srcn]], writes=[tb["SM"]])
    emit(P, "dve", lambda e: e.tensor_tensor(out=DB[:, ic, 0:w - 1], in0=SM[:, 0:w - 1], in1=UP[:, PADL:PADL + w - 1], op=ALU.subtract),
         reads=[tb["SM"], tb["B0"]], pwrites=[tb["DB"]])


def run_ME(j, inp, zT_list):
    nc = _get_nc(("ME",), build_ME)
    invc = np.tile((1.0 / np.arange(1, 17, dtype=np.float32))[None, :], (128, 1)).astype(np.float32)
    in_maps = []
    for c in range(NCORE):
        b, ch = c // 2, c % 2
        z = np.concatenate([zT_list[2 * b], zT_list[2 * b + 1]], axis=1)
        m = {}
        m["xrT"] = np.ascontiguousarray(z[ch * 512:(ch + 1) * 512])
        m["gT"] = np.ascontiguousarray(z[1024 + ch * 512:1024 + (ch + 1) * 512])
        m["uT"] = np.ascontiguousarray(z[2048:3072])
        vec = np.zeros((128, 4, 12), np.float32)
        for cc in range(4):
            chs = slice(ch * 512 + cc * 128, ch * 512 + (cc + 1) * 128)
            vec[:, cc, 0:4] = inp["ab_conv_w"][j][:, chs].T
            vec[:, cc, 4] = inp["ab_conv_b"][j][chs]
            vec[:, cc, 5] = inp["ab_b_rgate"][j][chs]
            vec[:, cc, 6] = inp["ab_b_igate"][j][chs]
            vec[:, cc, 7] = inp["ab_lambda"][j][chs]
            vec[:, cc, 8] = inp["ab_pool_scale"][j][cc * 256 + ch * 128:cc * 256 + (ch + 1) * 128]
        m["vecs"] = vec
        wbd = np.zeros((128, 2, 4, 128), np.float32)
        for gi, nm in enumerate(("ab_w_rgate", "ab_w_igate")):
            for cc in range(4):
                h0 = ch * 8 + 2 * cc
                wbd[0:64, gi, cc, 0:64] = inp[nm][j][h0]
                wbd[64:128, gi, cc, 64:128] = inp[nm][j][h0 + 1]
        m["wbd"] = wbd
        pw = np.zeros((128, 4, 2, 128), np.float32)
        for g in range(4):
            for ic in range(2):
                pw[:, g, ic, :] = inp["ab_pool_w"][j][g][ic * 128:(ic + 1) * 128, ch * 128:(ch + 1) * 128]
        m["pw"] = pw
        m["invc"] = invc
        in_maps.append(m)
    res = run_bass_kernel_spmd(nc, in_maps, core_ids=list(range(NCORE))).results
    out = []
    for b in range(B):
        full = np.zeros((D, S), res[0]["yT"].dtype)
        for ch in range(2):
            y = res[2 * b + ch]["yT"]
            full[ch * 512:(ch + 1) * 512] = y[0:512]
            for g in range(4):
                full[1024 + g * 256 + ch * 128:1024 + g * 256 + (ch + 1) * 128] = y[512 + g * 128:512 + (g + 1) * 128]
        out.append(np.ascontiguousarray(full[:, :TPC]))
        out.append(np.ascontiguousarray(full[:, TPC:]))
    return out


NQT = S // 128
SCL = 128.0 ** -0.5
NEGM = -30000.0
GC0 = 0.044715
GC1 = 1.5957691216057308


MO_FLAGS = dict(comp=True, att=True, nqt=NQT, ngl=2, gates=True)


def build_MO():
    FL = MO_FLAGS
    nc = bass.Bass("TRN2", target_bir_lowering=False)

    def din(name, shape, dt=F32):
        return nc.dram_tensor(name, shape, dt, kind="ExternalInput").ap()

    qT_d = din("qT", [2, 4, 128, S])
    kvT_d = din("kvT", [2, 6, 128, S])
    glT_d = din("glT", [24, S])
    w1_d = din("w1", [2, 32, 128, 512])
    w2_d = din("w2", [2, 512, 128])
    pos_d = din("pos", [128, 2, 32])
    rc_d = din("rc", [2, NQT, 10, 512], BF16)
    ls_d = din("ls", [74, 32, 128], BF16)
    lc_d = din("lc", [74, 2, 128], BF16)
    msk_d = din("msk", [128, 19, 128], BF16)
    ov_d = din("ov", [128, 2, 64], BF16)
    addt_d = din("addt", [128, NQT, 64])
    identf_d = din("identf", [128, 128])
    identb_d = din("identb", [128, 128], BF16)
    yT = nc.dram_tensor("yT", [1024, S], BF16, kind="ExternalOutput").ap()

    import contextlib
    with contextlib.ExitStack() as st:
        def sb(name, shape, dt):
            return st.enter_context(nc.sbuf_tensor(name, shape, dt))
        BIG = sb("BIG", [128, 4 * S], BF16)
        W1 = BIG[:].rearrange("p (l f) -> p l f", f=512)
        QT = BIG[:].rearrange("p (h t) -> p h t", t=S)
        KF = sb("KF", [128, S], F32)
        KA = sb("KA", [128, S], BF16)
        KB = sb("KB", [128, S], BF16)
        HID = sb("HID", [128, 4, 256], BF16)
        GT = [sb("GT%d" % i, [128, 256], F32) for i in range(2)]
        W2F = sb("W2F", [128, 2, 4, 128], F32)
        W2 = sb("W2", [128, 2, 4, 128], BF16)
        POS = sb("POS", [128, 2, 32], F32)
        KCT = sb("KCT", [128, 2, 256], BF16)
        VCA = sb("VCA", [128, 2, 2, 193], BF16)
        KST = sb("KST", [128, S], BF16)
        KWT = sb("KWT", [128, S], BF16)
        VSA = sb("VSA", [128, 32, 129], BF16)
        VWA = sb("VWA", [128, 32, 129], BF16)
        GSALL = sb("GSALL", [128, NQT, 24], F32)
        ADDT = sb("ADDT", [128, NQT, 64], F32)
        LS = sb("LS", [74, 32, 128], BF16)
        LC = sb("LC", [74, 2, 128], BF16)
        MSK = sb("MSK", [128, 19, 128], BF16)
        OVT = sb("OVT", [128, 2, 64], BF16)
        IDF = sb("IDF", [128, 128], F32)
        IDB = sb("IDB", [128, 128], BF16)
        R = [sb("R%d" % i, [74, 512], BF16) for i in range(2)]
        PTT = [sb("PTT%d" % i, [128, 512], BF16) for i in range(3)]
        O = [sb("O%d" % i, [128, 4, 128], F32) for i in range(2)]
        OTS = [sb("OTS%d" % i, [128, 4, 512], BF16) for i in range(2)]
        IMP = sb("IMP", [128, 64], F32)
        IMP3 = sb("IMP3", [128, 64], F32)
        NEG = sb("NEG", [128, 128], F32)
        T8 = sb("T8", [128, 16], F32)
        RZ = sb("RZ", [128, 3, 4], F32)
        CF = sb("CF", [128, 3, 4], F32)
        ps = st.enter_context(nc.psum_tensor("ps", [128, 8, 512], F32))

        P = Prog(nc)
        banks = Banks(ps, n=2)
        s_c = P.slot("const")
        s_w1 = P.slot("w1")
        s_kf = P.slot("kf")
        s_q = P.slot("q")
        s_k = P.slot("k")
        s_k2 = P.slot("k2")
        s_r = [P.slot("r0"), P.slot("r1")]
        s_o = [P.slot("o0"), P.slot("o1")]
        names = ["BIG", "KF", "KA", "KB", "HID", "GT0", "GT1", "KCT", "VCA", "KST", "KWT", "VSA", "VWA", "GSALL", "R0", "R1",
                 "PTT0", "PTT1", "PTT2", "O0", "O1", "OTS0", "OTS1", "IMP", "IMP3", "NEG", "T8", "RZ", "CF", "ACC0", "ACC1", "ACC2"]
        tb = {n: TB(None) for n in names}

        cl = None
        for (dst, src) in ((W2F[:], w2_d.rearrange("s (c p) d -> p s c d", p=128)), (POS[:], pos_d), (ADDT[:], addt_d), (LS[:], ls_d),
                           (LC[:], lc_d), (MSK[:], msk_d), (OVT[:], ov_d), (IDF[:], identf_d), (IDB[:], identb_d)):
            cl = P.dma("sync", s_c, lambda e, dst=dst, src=src: e.dma_start(out=dst, in_=src))
        kw2 = P.op("dve", lambda e: e.tensor_copy(out=W2[:], in_=W2F[:]), [cl])
        ones_ops = [P.op("dve", lambda e, V=V: e.memset(V[:, :, 128:129], 1.0)) for V in (VSA, VWA)]
        negz = P.op("dve", lambda e: e.memset(NEG[:], 0.0))
        tb["NEG"].w = [negz]
        ones_ops.append(P.op("dve", lambda e: e.memset(VCA[:], 0.0)))
        ones_ops.append(P.op("dve", lambda e: e.memset(VCA[:, :, :, 128:129], 1.0), [ones_ops[-1]]))
        for g_ in range(2):
            ones_ops.append(P.op("dve", lambda e, g_=g_: e.tensor_copy(out=VCA[:, g_, :, 129:193], in_=OVT[:]), [cl, ones_ops[-1]]))
        tb["VCA"].w = list(ones_ops[2:])
        tb["VSA"].w = [ones_ops[0]]
        tb["VWA"].w = [ones_ops[1]]

        emit(P, "sync", lambda e: e.dma_start(out=KF[0:24, :], in_=glT_d), writes=[tb["KF"]], slot=s_kf)
        begin_refill(tb["GSALL"])
        for i in (range(NQT) if FL['gates'] else []):
            b, fr = banks.get()
            tp = emit(P, "pe", lambda e, b=b, i=i: e.transpose(out=ps[:, b, 0:24], in_=KF[0:24, i * 128:(i + 1) * 128], identity=IDF[0:24, 0:24]),
                      reads=[tb["KF"]], extra=[fr, cl])
            ev = emit(P, "act", lambda e, b=b, i=i: e.activation(out=GSALL[:, i, :], in_=ps[:, b, 0:24], func=AF.Sigmoid), pwrites=[tb["GSALL"]], extra=[tp])
            banks.release(b, ev)
        end_refill(tb["GSALL"])

        for kv in (range(2) if FL['comp'] else []):
            emit(P, "pool", lambda e, kv=kv: e.dma_start(out=W1, in_=w1_d[kv].rearrange("l d f -> d l f")), writes=[tb["BIG"]], slot=s_w1)
            for gl in range(2):
                emit(P, "sync", lambda e, kv=kv, gl=gl: e.dma_start(out=KF[:], in_=kvT_d[gl, kv]), writes=[tb["KF"]], slot=s_kf)
                KF3 = KF[:].rearrange("p (n l) -> p n l", l=16)
                KA3 = KA[:].rearrange("p (n l) -> p n l", l=16)
                KB3 = KB[:].rearrange("p (n l) -> p n l", l=16)
                begin_refill(tb["KA"])
                begin_refill(tb["KB"])
                for l in range(16):
                    emit(P, "dve", lambda e, l=l, kv=kv: e.tensor_scalar(out=KA3[:, :, l], in0=KF3[:, :, l], scalar1=POS[:, kv, l:l + 1], scalar2=None, op0=ALU.add),
                         reads=[tb["KF"]], pwrites=[tb["KA"]], extra=[cl])
                    emit(P, "dve", lambda e, l=l, kv=kv: e.tensor_scalar(out=KB3[:, :, l], in0=KF3[:, :, l], scalar1=POS[:, kv, 16 + l:17 + l], scalar2=None, op0=ALU.add),
                         reads=[tb["KF"]], pwrites=[tb["KB"]], extra=[cl])
                end_refill(tb["KA"])
                end_refill(tb["KB"])
                begin_refill(tb["HID"])
                for fc in range(4):
                    b, fr = banks.get()
                    mm = None
                    for l in range(32):
                        rhs = KA3[:, 0:255, l] if l < 16 else KB3[:, 1:256, l - 16]
                        mm = emit(P, "pe", lambda e, b=b, l=l, fc=fc, rhs=rhs: e.matmul(ps[:, b, 0:255], W1[:, l, fc * 128:(fc + 1) * 128], rhs, start=(l == 0), stop=(l == 31)),
                                  reads=[tb["BIG"], tb["KA"], tb["KB"]], extra=[fr] if l == 0 else [])
                    g0, g1 = GT[0][:, 0:255], GT[1][:, 0:255]
                    emit(P, "act", lambda e, b=b, g0=g0: e.activation(out=g0, in_=ps[:, b, 0:255], func=AF.Square), writes=[tb["GT0"]], extra=[mm])
                    emit(P, "dve", lambda e, g0=g0: e.tensor_scalar(out=g0, in0=g0, scalar1=GC0, scalar2=1.0, op0=ALU.mult, op1=ALU.add), reads=[tb["GT0"]], writes=[tb["GT0"]])
                    emit(P, "dve", lambda e, b=b, g0=g0: e.tensor_tensor(out=g0, in0=ps[:, b, 0:255], in1=g0, op=ALU.mult), reads=[tb["GT0"]], writes=[tb["GT0"]], extra=[mm])
                    emit(P, "act", lambda e, g0=g0, g1=g1: e.activation(out=g1, in_=g0, func=AF.Sigmoid, scale=GC1), reads=[tb["GT0"]], writes=[tb["GT1"]])
                    ev = emit(P, "dve", lambda e, b=b, g1=g1, fc=fc: e.tensor_tensor(out=HID[:, fc, 0:255], in0=ps[:, b, 0:255], in1=g1, op=ALU.mult),
                              reads=[tb["GT1"]], pwrites=[tb["HID"]], extra=[mm])
                    banks.release(b, ev)
                end_refill(tb["HID"])
                if kv == 0:
                    b, fr = banks.get()
                    mm = None
                    for fc in range(4):
                        mm = emit(P, "pe", lambda e, b=b, fc=fc: e.matmul(ps[:, b, 0:255], W2[:, 0, fc, :], HID[:, fc, 0:255], start=(fc == 0), stop=(fc == 3)),
                                  reads=[tb["HID"]], extra=[fr, kw2] if fc == 0 else [])
                    ev = emit(P, "act", lambda e, b=b, gl=gl: e.copy(out=KCT[:, gl, 0:255], in_=ps[:, b, 0:255]), pwrites=[tb["KCT"]], extra=[mm])
                    banks.release(b, ev)
                else:
                    for cc in range(2):
                        nk = 128 if cc == 0 else 127
                        b, fr = banks.get()
                        mm = None
                        for fc in range(4):
                            mm = emit(P, "pe", lambda e, b=b, fc=fc, cc=cc, nk=nk: e.matmul(ps[0:nk, b, 0:128], HID[:, fc, cc * 128:cc * 128 + nk], W2[:, 1, fc, :], start=(fc == 0), stop=(fc == 3)),
                                      reads=[tb["HID"]], extra=[fr, kw2] if fc == 0 else [])
                        ev = emit(P, "act", lambda e, b=b, gl=gl, cc=cc, nk=nk: e.copy(out=VCA[0:nk, gl, cc, 0:128], in_=ps[0:nk, b, 0:128]), pwrites=[tb["VCA"]], extra=[mm])
                        banks.release(b, ev)

        out_waits = {}
        ots_n = 0
        def attn_group(gl):
            for h_ in range(4):
                emit(P, "pool", lambda e, gl=gl, h_=h_: e.dma_start(out=QT[:, h_, :].rearrange("p (a t) -> p a t", t=1024),
                                                                 in_=qT_d[gl, h_].rearrange("d (a t) -> d a t", t=1024)),
                     writes=[tb["BIG"]] if h_ == 0 else [], pwrites=[] if h_ == 0 else [tb["BIG"]], slot=s_q)
            emit(P, "pool", lambda e, gl=gl: e.dma_start(out=KST[:].rearrange("p (a t) -> p a t", t=1024),
                                                      in_=kvT_d[gl, 2].rearrange("d (a t) -> d a t", t=1024)), writes=[tb["KST"]], slot=s_k)
            emit(P, "pool", lambda e, gl=gl: e.dma_start(out=KWT[:].rearrange("p (a t) -> p a t", t=1024),
                                                      in_=kvT_d[gl, 4].rearrange("d (a t) -> d a t", t=1024)), writes=[tb["KWT"]], slot=s_k2)
            for (vi, VA, vn) in ((3, VSA, "VSA"), (5, VWA, "VWA")):
                emit(P, "sync", lambda e, gl=gl, vi=vi: e.dma_start(out=KF[:], in_=kvT_d[gl, vi]), writes=[tb["KF"]], slot=s_kf)
                begin_refill(tb[vn])
                for c4 in range(8):
                    b, fr = banks.get()
                    tp = None
                    for s_ in range(4):
                        c = c4 * 4 + s_
                        tp = emit(P, "pe", lambda e, b=b, s_=s_, c=c: e.transpose(out=ps[:, b, s_ * 128:(s_ + 1) * 128], in_=KF[:, c * 128:(c + 1) * 128], identity=IDF[:]),
                                  reads=[tb["KF"]], extra=[fr, cl] if s_ == 0 else [])
                    ev = emit(P, "act" if c4 % 2 == 0 else "dve",
                              (lambda e, b=b, c4=c4, VA=VA: e.copy(out=VA[:, c4 * 4:c4 * 4 + 4, 0:128], in_=ps[:, b, :].rearrange("p (s d) -> p s d", d=128))) if c4 % 2 == 0 else
                              (lambda e, b=b, c4=c4, VA=VA: e.tensor_copy(out=VA[:, c4 * 4:c4 * 4 + 4, 0:128], in_=ps[:, b, :].rearrange("p (s d) -> p s d", d=128))),
                              pwrites=[tb[vn]], extra=[tp])
                    banks.release(b, ev)
                end_refill(tb[vn])

            ptt_i = [0]

            def score_tile(i, nk, kT_ap, k_tbs, L_ap, R_ap, r_tb, krange, masks, pv_rhs, pv_tb, acc_bank0, ncol, first, last_, accn):
                b, fr = banks.get()
                k0, k1 = krange
                mm = emit(P, "pe", lambda e: e.matmul(ps[0:nk, b, :], kT_ap, QT[:, :, i * 128:(i + 1) * 128], start=True, stop=False),
                          reads=k_tbs + [tb["BIG"]], extra=[fr])
                nm = len(masks)
                mm = emit(P, "pe", lambda e: e.matmul(ps[0:nk, b, :], L_ap[k0:k1], R_ap[k0:k1, :], start=False, stop=(nm == 0)),
                          reads=[r_tb], extra=[cl])
                for mi, mk in enumerate(masks):
                    for h in range(4):
                        mm = emit(P, "pe", lambda e, h=h, mk=mk, mi=mi: e.matmul(ps[0:nk, b, h * 128:(h + 1) * 128], IDB[0:nk, 0:nk], MSK[0:nk, mk, :],
                                                                              start=False, stop=(mi == nm - 1 and h == 3)), extra=[cl])
                pi = ptt_i[0]
                ptt_i[0] = (pi + 1) % 3
                pt = PTT[pi]
                ex = emit(P, "act", lambda e: e.activation(out=pt[0:nk, :], in_=ps[0:nk, b, :], func=AF.Exp, scale=SCL), writes=[tb["PTT%d" % pi]], extra=[mm])
                banks.release(b, ex)
                for h in range(4):
                    bank = acc_bank0 + h // 2
                    col = (h % 2) * 256
                    st_flag = first and (h % 2 == 0)
                    emit(P, "pe", lambda e, h=h, bank=bank, col=col, st_flag=st_flag: e.matmul(
                        ps[:, bank, col:col + ncol], pt[0:nk, h * 128:(h + 1) * 128], pv_rhs, start=st_flag, stop=last_, skip_group_check=True),
                        reads=[tb["PTT%d" % pi], pv_tb], writes=[tb[accn]] if (first and h == 0) else [], pwrites=[] if (first and h == 0) else [tb[accn]])

            def acc_ap(bank0, h, c0, c1):
                return ps[:, bank0 + h // 2, (h % 2) * 256 + c0:(h % 2) * 256 + c1]

            def zr(bank0, br):
                for h in range(4):
                    emit(P, "dve", lambda e, h=h: e.tensor_scalar(out=RZ[:, br, h:h + 1], in0=acc_ap(bank0, h, 128, 129), scalar1=1e-30, scalar2=None, op0=ALU.max),
                         reads=[tb["ACC%d" % br]], pwrites=[tb["RZ"]])
                emit(P, "dve", lambda e: e.reciprocal(out=RZ[:, br, :], in_=RZ[:, br, :]), reads=[tb["RZ"]], pwrites=[tb["RZ"]])

            def load_R(i):
                ri = i % 2
                return emit(P, "sync", lambda e: e.dma_start(out=R[ri][64:74, :], in_=rc_d[gl, i]), writes=[tb["R%d" % ri]], slot=s_r[ri])

            def cmp_branch(i):
                ri = i % 2
                ccs = [0] if i < 16 else [0, 1]
                for ci, cc in enumerate(ccs):
                    nk = 128 if cc == 0 else 127
                    ip = i - 16 * cc
                    masks = [2 + ip] if ip <= 16 else []
                    score_tile(i, nk, KCT[:, gl, cc * 128:cc * 128 + nk], [tb["KCT"]], LC[:, cc, 0:nk], R[ri], tb["R%d" % ri], (64, 74), masks,
                               VCA[0:nk, gl, cc, :], tb["VCA"], 2, 193, ci == 0, ci == len(ccs) - 1, "ACC0")
                oi = i % 2
                tbo = tb["O%d" % oi]
                begin_refill(tb["RZ"])
                zr(2, 0)
                emit(P, "dve", lambda e: e.tensor_scalar(out=IMP[:], in0=acc_ap(2, 0, 129, 193), scalar1=RZ[:, 0, 0:1], scalar2=None, op0=ALU.mult),
                     reads=[tb["ACC0"], tb["RZ"]], writes=[tb["IMP"]])
                for h in range(1, 4):
                    emit(P, "dve", lambda e, h=h: e.scalar_tensor_tensor(out=IMP[:], in0=acc_ap(2, h, 129, 193), scalar=RZ[:, 0, h:h + 1], in1=IMP[:], op0=ALU.mult, op1=ALU.add),
                         reads=[tb["ACC0"], tb["RZ"]], writes=[tb["IMP"]])
                emit(P, "dve", lambda e: e.tensor_tensor(out=IMP[:], in0=IMP[:], in1=ADDT[:, i, :], op=ALU.add), writes=[tb["IMP"]], extra=[cl])
                emit(P, "dve", lambda e: e.max(out=T8[:, 0:8], in_=IMP[:]), reads=[tb["IMP"]], writes=[tb["T8"]])
                emit(P, "dve", lambda e: e.match_replace(out=IMP3[:], in_to_replace=T8[:, 0:8], in_values=IMP[:], imm_value=-3.0e38),
                     reads=[tb["IMP"], tb["T8"]], writes=[tb["IMP3"]])
                emit(P, "dve", lambda e: e.max(out=T8[:, 8:16], in_=IMP3[:]), reads=[tb["IMP3"]], pwrites=[tb["T8"]])
                emit(P, "dve", lambda e: e.tensor_scalar(out=NEG[:, 0:64], in0=IMP[:], scalar1=T8[:, 15:16], scalar2=NEGM, op0=ALU.is_lt, op1=ALU.mult),
                     reads=[tb["IMP"], tb["T8"]], pwrites=[tb["NEG"]])
                b, fr = banks.get()
                tp = emit(P, "pe", lambda e: e.transpose(out=ps[:, b, 0:128], in_=NEG[:], identity=IDF[:]), reads=[tb["NEG"]], extra=[fr, cl])
                ev = None
                for h in range(4):
                    ev = emit(P, "act", lambda e, h=h: e.copy(out=R[ri][0:64, h * 128:(h + 1) * 128], in_=ps[0:64, b, 0:128]), pwrites=[tb["R%d" % ri]], extra=[tp])
                banks.release(b, ev)
                emit(P, "dve", lambda e: e.tensor_tensor(out=CF[:, 0, :], in0=RZ[:, 0, :], in1=GSALL[:, i, gl * 12:gl * 12 + 4], op=ALU.mult),
                     reads=[tb["RZ"], tb["GSALL"]], writes=[tb["CF"]])
                for h in range(4):
                    emit(P, "dve", lambda e, h=h: e.tensor_scalar(out=O[oi][:, h, :], in0=acc_ap(2, h, 0, 128), scalar1=CF[:, 0, h:h + 1], scalar2=None, op0=ALU.mult),
                         reads=[tb["ACC0"], tb["CF"]], writes=[tbo] if h == 0 else [], pwrites=[] if h == 0 else [tbo])

            def dense_branch(i, br):
                ri = i % 2
                oi = i % 2
                tbo = tb["O%d" % oi]
                if br == 1:
                    cs = list(range(0, i + 1))
                    kT, ktb, VA, vtb, kr, bank0 = KST, tb["KST"], VSA, tb["VSA"], (0, 74), 4
                else:
                    cs = list(range(max(0, i - 4), i + 1))
                    kT, ktb, VA, vtb, kr, bank0 = KWT, tb["KWT"], VWA, tb["VWA"], (64, 74), 6
                for ci, c in enumerate(cs):
                    masks = []
                    if c == i:
                        masks.append(0)
                    if br == 2 and c == i - 4:
                        masks.append(1)
                    score_tile(i, 128, kT[:, c * 128:(c + 1) * 128], [ktb], LS[:, c, :], R[ri], tb["R%d" % ri], kr, masks,
                               VA[:, c, :], vtb, bank0, 129, ci == 0, ci == len(cs) - 1, "ACC%d" % br)
                zr(bank0, br)
                emit(P, "dve", lambda e: e.tensor_tensor(out=CF[:, br, :], in0=RZ[:, br, :], in1=GSALL[:, i, gl * 12 + br * 4:gl * 12 + br * 4 + 4], op=ALU.mult),
                     reads=[tb["RZ"], tb["GSALL"]], pwrites=[tb["CF"]])
                for h in range(4):
                    emit(P, "dve", lambda e, h=h: e.scalar_tensor_tensor(out=O[oi][:, h, :], in0=acc_ap(bank0, h, 0, 128), scalar=CF[:, br, h:h + 1], in1=O[oi][:, h, :],
                                                                      op0=ALU.mult, op1=ALU.add), reads=[tb["ACC%d" % br], tb["CF"]], pwrites=[tbo])

            def finish(i):
                oi = i % 2
                q4 = i % 4
                osi = (i // 4) % 2
                tbs = tb["OTS%d" % osi]
                if q4 == 0:
                    begin_refill(tbs)
                for h2 in range(2):
                    b, fr = banks.get()
                    tp = None
                    for hh in range(2):
                        h = h2 * 2 + hh
                        tp = emit(P, "pe", lambda e, h=h, hh=hh, b=b: e.transpose(out=ps[:, b, hh * 128:(hh + 1) * 128], in_=O[oi][:, h, :], identity=IDF[:]),
                                  reads=[tb["O%d" % oi]], extra=[fr, cl] if hh == 0 else [])
                    if h2 == 0:
                        ev = emit(P, "act", lambda e, b=b, h2=h2: e.copy(out=OTS[osi][:, 0:2, q4 * 128:(q4 + 1) * 128], in_=ps[:, b, 0:256].rearrange("p (h t) -> p h t", t=128)),
                                  pwrites=[tbs], extra=[tp])
                    else:
                        ev = emit(P, "dve", lambda e, b=b, h2=h2: e.tensor_copy(out=OTS[osi][:, 2:4, q4 * 128:(q4 + 1) * 128], in_=ps[:, b, 0:256].rearrange("p (h t) -> p h t", t=128)),
                                  pwrites=[tbs], extra=[tp])
                    banks.release(b, ev)
                if q4 == 3:
                    end_refill(tbs)
                    t0 = (i // 4) * 512
                    out_waits[osi] = emit(P, "sync", lambda e, t0=t0: e.dma_start(
                        out=yT[gl * 512:(gl + 1) * 512, t0:t0 + 512].rearrange("(h d) t -> d h t", d=128), in_=OTS[osi][:]), reads=[tbs], slot=s_o[osi])

            if FL.get('stop') == 'loads':
                return
            load_R(0)
            cmp_branch(0)
            for i in range(FL['nqt']):
                if i + 1 < FL['nqt']:
                    load_R(i + 1)
                    cmp_branch(i + 1)
                dense_branch(i, 1)
                dense_branch(i, 2)
                finish(i)
        for gl_ in (range(FL['ngl']) if FL['att'] else []):
            attn_group(gl_)
        P.op("sync", lambda e: e.nop(), list(out_waits.values()))
        P.finalize()
    return nc


def _bf(x):
    return np.asarray(x, np.float32).astype(ml_dtypes.bfloat16)


def _split(x, n):
    r = np.asarray(x, np.float64)
    parts = []
    for _ in range(n):
        p = _bf(r)
        parts.append(p)
        r = r - p.astype(np.float64)
    return parts


_MO_CONST = {}


def _mo_consts(gp):
    if gp in _MO_CONST:
        return _MO_CONST[gp]
    isc = 1.0 / SCL
    rc = np.zeros((2, NQT, 10, 512), ml_dtypes.bfloat16)
    j = np.arange(128)
    for gl in range(2):
        g = 2 * gp + gl
        for hh in range(4):
            slope = 2.0 ** (-8.0 * (g * 4 + hh + 1) / 16.0)
            s2 = _split(np.full(128, slope * isc), 2)
            for i in range(NQT):
                t = 128 * i + j
                cs = slice(hh * 128, (hh + 1) * 128)
                a = _split(-slope * t * isc, 3)
                ac = _split(-slope * (t - 31) * isc, 3)
                for r_ in range(3):
                    rc[gl, i, r_, cs] = a[r_]
                    rc[gl, i, 7 + r_, cs] = ac[r_]
                rc[gl, i, 3, cs] = s2[0]
                rc[gl, i, 4, cs] = s2[1]
                rc[gl, i, 5, cs] = s2[0]
                rc[gl, i, 6, cs] = s2[1]
    k = np.arange(128)
    ls = np.zeros((74, 32, 128), np.float32)
    for c in range(32):
        ls[2 * c, c, 0:64] = 1.0
        ls[2 * c + 1, c, 64:128] = 1.0
        ls[64:67, c, :] = 1.0
        ls[67:69, c, :] = k[None, :]
        ls[69:71, c, :] = 128.0 * c
    lc = np.zeros((74, 2, 128), np.float32)
    for cc in range(2):
        lc[67:69, cc, :] = 16.0 * k[None, :]
        lc[69:71, cc, :] = 2048.0 * cc
        lc[71:74, cc, :] = 1.0
    msk = np.zeros((128, 19, 128), np.float32)
    kk = k[:, None]
    jj = j[None, :]
    msk[:, 0, :] = np.where(jj < kk, NEGM, 0.0)
    msk[:, 1, :] = np.where(jj >= kk, NEGM, 0.0)
    for ip in range(17):
        msk[:, 2 + ip, :] = np.where(128 * ip + jj < 16 * kk + 31, NEGM, 0.0)
    ov = np.zeros((128, 2, 64), np.float32)
    m = np.arange(64)[None, :]
    for cc in range(2):
        n = (cc * 128 + k)[:, None]
        ov[:, cc, :] = ((16 * n < 64 * m + 64) & (16 * n + 32 > 64 * m) & (n <= 254)).astype(np.float32)
    addt = np.zeros((128, NQT, 64), np.float32)
    for i in range(NQT):
        t = (128 * i + j)[:, None]
        cur = t // 64
        mm = np.arange(64)[None, :]
        forced = (mm == 0) | (mm == cur) | (mm == cur - 1)
        causal = (64 * mm) <= t
        addt[:, i, :] = np.where(causal, np.where(forced, 1.0e6, 0.0), -1.0e30)
    out = dict(rc=rc, ls=_bf(ls), lc=_bf(lc), msk=_bf(msk), ov=_bf(ov), addt=addt,
               identf=np.eye(128, dtype=np.float32), identb=_bf(np.eye(128)))
    _MO_CONST[gp] = out
    return out


def run_MO(jl, inp, zT_list):
    nc = _get_nc(("MO",), build_MO)
    w1 = np.stack([inp["c_cmp_w1_k"][jl], inp["c_cmp_w1_v"][jl]])
    w2 = np.stack([inp["c_cmp_w2_k"][jl], inp["c_cmp_w2_v"][jl]])
    pos = np.ascontiguousarray(np.stack([inp["c_cmp_pos_k"][jl].T, inp["c_cmp_pos_v"][jl].T], axis=1))
    in_maps = []
    for c in range(NCORE):
        b, gp = c // 2, c % 2
        z = np.concatenate([zT_list[2 * b], zT_list[2 * b + 1]], axis=1)
        m = dict(_mo_consts(gp))
        m["qT"] = np.ascontiguousarray(z[0:2048].reshape(4, 4, 128, S)[2 * gp:2 * gp + 2])
        m["kvT"] = np.ascontiguousarray(z[2048:5120].reshape(6, 4, 128, S)[:, 2 * gp:2 * gp + 2].transpose(1, 0, 2, 3))
        m["glT"] = np.ascontiguousarray(z[5120:5168].reshape(3, 4, 4, S)[:, 2 * gp:2 * gp + 2].transpose(1, 0, 2, 3).reshape(24, S))
        m["w1"] = w1
        m["w2"] = w2
        m["pos"] = pos
        in_maps.append(m)
    res = run_bass_kernel_spmd(nc, in_maps, core_ids=list(range(NCORE))).results
    out = []
    for b in range(B):
        full = np.concatenate([res[2 * b]["yT"], res[2 * b + 1]["yT"]], axis=0)
        out.append(np.ascontiguousarray(full[:, :TPC]))
        out.append(np.ascontiguousarray(full[:, TPC:]))
    return out


def kernel(**inputs):
    inp = {k: np.asarray(v) for k, v in inputs.items()}
    res = run_PA(None, 0, inp, None, None)
    hT = [r["hT_out"] for r in res]
    zT = [r["zT"] for r in res]
    for i in range(DEPTH):
        if i % 2 == 0:
            yT = run_ME(i // 2, inp, zT)
        else:
            yT = run_MO(i // 2, inp, zT)
        nxt = i + 1 if i + 1 < DEPTH else None
        res = run_PA(i, nxt, inp, hT, yT)
        if nxt is not None:
            hT = [r["hT_out"] for r in res]
            zT = [r["zT"] for r in res]
    out = np.zeros((B, S, D), np.float32)
    for c in range(NCORE):
        b, hf = c // 2, c % 2
        out[b, hf * TPC:(hf + 1) * TPC] = res[c]["out_tok"]
    return out
```
